# Optimizing a Trainium2 kernel written in Bass

```python
import jax, jax.numpy as jnp
from jax import lax
import numpy as np

D_MODEL = 1024
BATCH = 4
SEQ = 8192
DEPTH = 2

CHUNK = 64
Q_BLOCK = 128
CONV_WIDTH = 31
N_HEADS = 16
HEAD_DIM = D_MODEL // N_HEADS
ATTN_WIDTH = N_HEADS * HEAD_DIM
D_FF = -(-(8 * D_MODEL) // (3 * 256)) * 256
N_A_LAYERS = DEPTH // 2
N_B_LAYERS = DEPTH - N_A_LAYERS
EPS = 1e-6
FORGET_BIAS_MEAN = 2.0

kernel_name = "yoco_conformer_fox_adaln_trunk"


def _rmsnorm(x, g):
    x32 = x.astype(jnp.float32)
    y = x32 * lax.rsqrt(jnp.mean(x32 * x32, axis=-1, keepdims=True) + EPS)
    return (y * g.astype(jnp.float32)).astype(x.dtype)


def _layernorm(x, g, b):
    x32 = x.astype(jnp.float32)
    mu = jnp.mean(x32, axis=-1, keepdims=True)
    var = jnp.mean(jnp.square(x32 - mu), axis=-1, keepdims=True)
    y = (x32 - mu) * lax.rsqrt(var + EPS)
    return (y * g.astype(jnp.float32) + b.astype(jnp.float32)).astype(x.dtype)


def _modulate(h, shift, scale):
    return h * (1.0 + scale[:, None, :]) + shift[:, None, :]


def _conformer_conv(h, w_in, b_in, w_dw, b_dw, ln_g, ln_b, w_out, b_out):
    u = h @ w_in + b_in
    a, g = jnp.split(u, 2, axis=-1)
    u = a * jax.nn.sigmoid(g)
    d = u.shape[-1]
    u = lax.conv_general_dilated(
        u, w_dw[:, None, :].astype(u.dtype),
        window_strides=(1,), padding=[(CONV_WIDTH - 1, 0)],
        dimension_numbers=("NWC", "WIO", "NWC"),
        feature_group_count=d) + b_dw
    u = jax.nn.silu(_layernorm(u, ln_g, ln_b))
    return u @ w_out + b_out


def _forgetting_attention(q, k, v, cum):
    b, s, h, hd = q.shape
    n_blk = s // Q_BLOCK
    scale = hd ** -0.5
    qb = q.reshape(b, n_blk, Q_BLOCK, h, hd).transpose(1, 0, 2, 3, 4)
    cum_h = cum.transpose(0, 2, 1)
    cqb = cum_h.reshape(b, h, n_blk, Q_BLOCK).transpose(2, 0, 1, 3)
    key_pos = jnp.arange(s)

    def one_block(args):
        q_i, cq_i, i = args
        logits = jnp.einsum("bqhd,bkhd->bhqk", q_i, k).astype(jnp.float32) * scale
        logits = logits + (cq_i[..., :, None] - cum_h[:, :, None, :])
        q_pos = i * Q_BLOCK + jnp.arange(Q_BLOCK)
        mask = key_pos[None, :] <= q_pos[:, None]
        logits = jnp.where(mask[None, None], logits, -jnp.inf)
        p = jax.nn.softmax(logits, axis=-1)
        return jnp.einsum("bhqk,bkhd->bqhd", p.astype(v.dtype), v)

    out = lax.map(one_block, (qb, cqb, jnp.arange(n_blk)))
    return out.transpose(1, 0, 2, 3, 4).reshape(b, s, h, hd)


def setup_inputs(seed: int = 0) -> dict:
    key = jax.random.key(seed)
    ks = iter(jax.random.split(key, 32))
    D, F, K, H = D_MODEL, D_FF, CONV_WIDTH, N_HEADS

    def nrm(shape, std):
        return jax.random.normal(next(ks), shape, jnp.float32) * std

    def gain(shape):
        return 1.0 + nrm(shape, 0.02)

    return {
        "x": nrm((BATCH, SEQ, D), 1.0),
        "c": nrm((BATCH, D), 1.0),
        "mix_norm_g": gain((DEPTH, D)),
        "mix_ada_w": nrm((DEPTH, D, 3 * D), 0.5 * D ** -0.5),
        "mix_ada_b": nrm((DEPTH, 3 * D), 0.02),
        "ffn_norm_g": gain((DEPTH, D)),
        "ffn_ada_w": nrm((DEPTH, D, 3 * D), 0.5 * D ** -0.5),
        "ffn_ada_b": nrm((DEPTH, 3 * D), 0.02),
        "ffn_w_in": nrm((DEPTH, D, 2 * F), D ** -0.5),
        "ffn_w_out": nrm((DEPTH, F, D), F ** -0.5),
        "conv_w_in": nrm((N_A_LAYERS, D, 2 * D), D ** -0.5),
        "conv_b_in": nrm((N_A_LAYERS, 2 * D), 0.02),
        "conv_w_dw": nrm((N_A_LAYERS, K, D), K ** -0.5),
        "conv_b_dw": nrm((N_A_LAYERS, D), 0.02),
        "conv_ln_g": gain((N_A_LAYERS, D)),
        "conv_ln_b": nrm((N_A_LAYERS, D), 0.02),
        "conv_w_out": nrm((N_A_LAYERS, D, D), D ** -0.5),
        "conv_b_out": nrm((N_A_LAYERS, D), 0.02),
        "kv_norm_g": gain((D,)),
        "kv_ada_w": nrm((D, 2 * D), 0.5 * D ** -0.5),
        "kv_ada_b": nrm((2 * D,), 0.02),
        "kv_w": nrm((D, 2 * ATTN_WIDTH + H), D ** -0.5),
        "forget_b": FORGET_BIAS_MEAN + nrm((H,), 0.1),
        "attn_w_q": nrm((N_B_LAYERS, D, ATTN_WIDTH), D ** -0.5),
        "attn_w_o": nrm((N_B_LAYERS, ATTN_WIDTH, D), ATTN_WIDTH ** -0.5),
        "final_norm_g": gain((D,)),
    }


def reference(x, c, mix_norm_g, mix_ada_w, mix_ada_b, ffn_norm_g, ffn_ada_w, ffn_ada_b,
              ffn_w_in, ffn_w_out, conv_w_in, conv_b_in, conv_w_dw, conv_b_dw, conv_ln_g,
              conv_ln_b, conv_w_out, conv_b_out, kv_norm_g, kv_ada_w, kv_ada_b, kv_w,
              forget_b, attn_w_q, attn_w_o, final_norm_g):
    b, s, _ = x.shape
    c_act = jax.nn.silu(c)

    def ada(w, bias, n):
        return jnp.split(c_act @ w + bias, n, axis=-1)

    k_sh = v_sh = cum_sh = None
    for layer in range(DEPTH):
        shift, scale, gate = ada(mix_ada_w[layer], mix_ada_b[layer], 3)
        h = _modulate(_rmsnorm(x, mix_norm_g[layer]), shift, scale)
        if layer < N_A_LAYERS:
            i = layer
            y = _conformer_conv(h, conv_w_in[i], conv_b_in[i], conv_w_dw[i], conv_b_dw[i],
                                conv_ln_g[i], conv_ln_b[i], conv_w_out[i], conv_b_out[i])
        else:
            j = layer - N_A_LAYERS
            q = (h @ attn_w_q[j]).reshape(b, s, N_HEADS, HEAD_DIM)
            o = _forgetting_attention(q, k_sh, v_sh, cum_sh)
            y = o.reshape(b, s, ATTN_WIDTH) @ attn_w_o[j]
        x = x + gate[:, None, :] * y

        shift, scale, gate = ada(ffn_ada_w[layer], ffn_ada_b[layer], 3)
        h = _modulate(_rmsnorm(x, ffn_norm_g[layer]), shift, scale)
        u_gate, u_up = jnp.split(h @ ffn_w_in[layer], 2, axis=-1)
        x = x + gate[:, None, :] * ((jax.nn.silu(u_gate) * u_up) @ ffn_w_out[layer])

        if layer == N_A_LAYERS - 1:
            shift, scale = ada(kv_ada_w, kv_ada_b, 2)
            hk = _modulate(_rmsnorm(x, kv_norm_g), shift, scale)
            kvf = hk @ kv_w
            k_sh = kvf[..., :ATTN_WIDTH].reshape(b, s, N_HEADS, HEAD_DIM)
            v_sh = kvf[..., ATTN_WIDTH:2 * ATTN_WIDTH].reshape(b, s, N_HEADS, HEAD_DIM)
            f_logit = (kvf[..., 2 * ATTN_WIDTH:] + forget_b).astype(jnp.float32)
            cum_sh = jnp.cumsum(jax.nn.log_sigmoid(f_logit), axis=1)

    return _rmsnorm(x, final_norm_g)
```

```python
import numpy as np
import ml_dtypes
import concourse.bass as bass
import concourse.mybir as mybir
from concourse.bass_utils import run_bass_kernel_spmd

F32 = mybir.dt.float32
BF16 = mybir.dt.bfloat16
AF = mybir.ActivationFunctionType
ALU = mybir.AluOpType

D = 1024
S = 8192
B = 4
F = 2816
NH = 16
KW = 31
EPS = 1e-6
NT = 8
TT = 512
HALO = 32
CH = [(0, 3, 4, 7), (1, 2, 5, 6)]
MI = (1, 3, 5, 7)
GCH = {0: (0, 0), 1: (1, 0), 2: (1, 1), 3: (0, 1), 4: (0, 2), 5: (1, 2), 6: (1, 3), 7: (0, 3)}
RING = 4
SLOT = 5632


class Sched:
    def __init__(self, nc):
        self.nc = nc
        self.ops = []
        self.lw = {}
        self.rd = {}
        self.dcount = {}

    def op(self, eng, fn, r=(), w=(), dsem=None, short=False):
        i = len(self.ops)
        deps = set()
        for k in r:
            if k in self.lw:
                deps.add(self.lw[k])
        for k in w:
            if k in self.lw:
                deps.add(self.lw[k])
            rk = self.rd.get(k)
            if rk:
                deps.update(rk[0].values())
                deps.update(rk[1])
        o = dict(eng=eng, fn=fn, deps=deps, dsem=dsem, sig=None, need=False, dval=None, short=short)
        if dsem is not None:
            self.dcount[dsem] = self.dcount.get(dsem, 0) + 16
            o["dval"] = self.dcount[dsem]
        self.ops.append(o)
        for k in r:
            rk = self.rd.setdefault(k, ({}, []))
            if dsem is None:
                rk[0][eng] = i
            else:
                rk[1].append(i)
        for k in w:
            self.lw[k] = i
            self.rd[k] = ({}, [])
        return i

    def emit(self, final_waits=()):
        nc = self.nc
        ops = self.ops
        for o in ops:
            for d in o["deps"]:
                Dd = ops[d]
                if Dd["dsem"] is None and (Dd["eng"] != o["eng"] or Dd["short"]):
                    Dd["need"] = True
        cnt = {}
        for o in ops:
            if o["dsem"] is None and o["need"]:
                cnt[o["eng"]] = cnt.get(o["eng"], 0) + 1
                o["sig"] = cnt[o["eng"]]
        engs = ["pe", "act", "dve", "pool", "sp"]
        import contextlib
        with contextlib.ExitStack() as es:
            esem = {e: es.enter_context(nc.semaphore("e_" + e)) for e in engs}
            dsem = {k: es.enter_context(nc.semaphore("d_" + str(k))) for k in self.dcount}
            block = es.enter_context(nc.Block())
            per = {e: [o for o in ops if o["eng"] == e] for e in engs}

            def run(e, h):
                waited = {}
                for o in per[e]:
                    for d in sorted(o["deps"]):
                        Dd = ops[d]
                        if Dd["dsem"] is not None:
                            key = ("d", Dd["dsem"])
                            sem = dsem[Dd["dsem"]]
                            val = Dd["dval"]
                        elif Dd["eng"] != e or Dd["short"]:
                            key = ("e", Dd["eng"])
                            sem = esem[Dd["eng"]]
                            val = Dd["sig"]
                        else:
                            continue
                        if waited.get(key, 0) < val:
                            h.wait_ge(sem, val)
                            waited[key] = val
                    inst = o["fn"](h)
                    if o["dsem"] is not None:
                        inst.then_inc(dsem[o["dsem"]], 16)
                    elif o["need"]:
                        inst.then_inc(esem[e], 1)
                if e == "sp":
                    for k in final_waits:
                        h.wait_ge(dsem[k], self.dcount[k])

            @block.tensor
            def _(h):
                run("pe", h)

            @block.scalar
            def _(h):
                run("act", h)

            @block.vector
            def _(h):
                run("dve", h)

            @block.gpsimd
            def _(h):
                run("pool", h)

            @block.sync
            def _(h):
                run("sp", h)


class Ctx:
    pass


def wview(w):
    return w.rearrange("(k p) m -> p k m", p=128)


class WStream:
    def __init__(self, g):
        self.g = g
        self.plan = []
        self.pos = 0
        self.issued = 0

    def issue(self):
        g = self.g
        n = self.issued
        if n >= len(self.plan):
            return
        parts = self.plan[n]
        slot = n % RING
        for pi, (off, shape, src, key) in enumerate(parts):
            k, m = shape
            dst = g.ring[:, slot, off:off + k * m].rearrange("p (k m) -> p k m", k=k)
            g.s.op("sp", (lambda h, dst=dst, src=src: h.dma_start(out=dst, in_=src)),
                   r=[key] if key else [], w=[("ring", slot, pi)], dsem=("ring", slot))
        self.issued += 1

    def start(self):
        for _ in range(RING):
            self.issue()

    def get(self, parts):
        if self.g.planning:
            self.plan.append(parts)
            return 0
        n = self.pos
        self.pos += 1
        return n % RING

    def release(self):
        if not self.g.planning:
            self.issue()


def slab(g, slot, off, k, m):
    return g.ring[:, slot, off:off + k * m].rearrange("p (k m) -> p k m", k=k)


class PS:
    def __init__(self, g, banks):
        self.g = g
        self.banks = list(banks)
        self.i = 0

    def next(self):
        b = self.banks[self.i % len(self.banks)]
        self.i += 1
        return b


def mm(g, bank, cols, lhsT, rhs, start, stop, r, tp=None):
    out = g.ps[bank][:, cols[0]:cols[1]]
    if tp is None:
        fn = lambda h: h.matmul(out, lhsT=lhsT, rhs=rhs, start=start, stop=stop)
    else:
        fn = lambda h: h.matmul(out, lhsT=lhsT, rhs=rhs, start=start, stop=stop, tile_position=tp)
    g.s.op("pe", fn, r=r, w=[("ps", bank)])


def norm_mod(g, xb, groups, gs, sh, hb, hoff, final_out=None, gvec=None):
    s = g.s
    for (lo, hi) in groups:
        w = hi - lo
        sh_ = w <= 128
        bank = g.psr.next()
        for c in range(8):
            j = g.sqi % 2
            g.sqi += 1
            sq = g.sq[:, j, 0:w]
            xin = xb[:, c, lo:hi]
            s.op("dve", lambda h, sq=sq, xin=xin: h.tensor_tensor(out=sq, in0=xin, in1=xin, op=ALU.mult),
                 r=[("x", c)], w=[("sq", j)], short=sh_)
            mm(g, bank, (0, w), g.ones_bf[:, :], sq, c == 0, c == 7, r=[("sq", j), "const"])
        psv = g.ps[bank][:, 0:w]
        ta = g.tmpa[:, 0:w]
        s.op("act", lambda h, ta=ta, psv=psv: h.activation(out=ta, in_=psv, func=AF.Sqrt, bias=g.epsc[:, 0:1], scale=1.0 / D),
             r=[("ps", bank), "const"], w=["tmpa"], short=sh_)
        rs = g.rstd[:, 0:w]
        s.op("dve", lambda h, rs=rs, ta=ta: h.reciprocal(out=rs, in_=ta), r=["tmpa"], w=["rstd"], short=True)
        for c in range(8):
            xin = xb[:, c, lo:hi]
            if final_out is not None:
                s.op("dve", lambda h, xin=xin, rs=rs, c=c: h.scalar_tensor_tensor(
                    out=xin, in0=xin, scalar=gvec[:, c:c + 1], in1=rs, op0=ALU.mult, op1=ALU.mult),
                    r=[("x", c), "rstd", "vecs"], w=[("x", c)], short=sh_)
                continue
            j = g.tbi % 2
            g.tbi += 1
            tb = g.tmpb[:, j, 0:w]
            s.op("dve", lambda h, tb=tb, xin=xin, rs=rs: h.tensor_tensor(out=tb, in0=xin, in1=rs, op=ALU.mult),
                 r=[("x", c), "rstd"], w=[("tmpb", j)], short=sh_)
            ho = hb[:, c, lo - hoff:hi - hoff]
            s.op("act", lambda h, ho=ho, tb=tb, c=c: h.activation(
                out=ho, in_=tb, func=AF.Identity, bias=sh[:, c:c + 1], scale=gs[:, c:c + 1]),
                r=[("tmpb", j), "ada"], w=[("h", c)], short=sh_)


def ada_compute(g, wdram, nm, bias_cols, out_tile):
    s = g.s
    bank = g.psr.next()
    wv = wview(wdram)
    for m in range(nm):
        j = g.adai % 2
        g.adai += 1
        dst = g.adaw[:, j, :, :]
        src = wv[:, :, m * 128:(m + 1) * 128]
        s.op("sp", lambda h, dst=dst, src=src: h.dma_start(out=dst, in_=src), w=[("adaw", j)], dsem=("adaw", j))
        for k in range(8):
            mm(g, bank, (m, m + 1), g.adaw[:, j, k, :], g.cact[:, k:k + 1], k == 0, k == 7,
               r=[("adaw", j), "cact"])
    psv = g.ps[bank][:, 0:nm]
    s.op("dve", lambda h: h.tensor_tensor(out=out_tile[:, 0:nm], in0=psv, in1=bias_cols, op=ALU.add),
         r=[("ps", bank), "vecs"], w=["ada"], short=True)


def ffn(g, xb, xo, w_in_b, w_out_b, gate, wkeys):
    s = g.s
    wi = wview(w_in_b)
    for sl in range(11):
        parts = [(0, (8, 256), wi[:, :, sl * 256:(sl + 1) * 256], wkeys[0]),
                 (2048, (8, 256), wi[:, :, F + sl * 256:F + (sl + 1) * 256], wkeys[0])]
        slot = g.ws.get(parts)
        if not g.planning:
            wg = slab(g, slot, 0, 8, 256)
            wu = slab(g, slot, 2048, 8, 256)
            for j in range(2):
                bg = g.psr.next()
                bu = g.psr.next()
                for k in range(8):
                    mm(g, bg, (0, TT), wg[:, k, j * 128:(j + 1) * 128], g.hb[:, k, 0:TT], k == 0, k == 7,
                       r=[("ring", slot, 0), ("h", k)])
                for k in range(8):
                    mm(g, bu, (0, TT), wu[:, k, j * 128:(j + 1) * 128], g.hb[:, k, 0:TT], k == 0, k == 7,
                       r=[("ring", slot, 1), ("h", k)])
                ti = g.tci % 2
                g.tci += 1
                tc_ = g.tmpc[:, ti, :]
                pg = g.ps[bg][:, :]
                pu = g.ps[bu][:, :]
                s.op("act", lambda h, tc_=tc_, pg=pg: h.activation(out=tc_, in_=pg, func=AF.Silu),
                     r=[("ps", bg)], w=[("tmpc", ti)])
                ao = g.act[:, sl * 2 + j, :]
                s.op("dve", lambda h, ao=ao, tc_=tc_, pu=pu: h.tensor_tensor(out=ao, in0=tc_, in1=pu, op=ALU.mult),
                     r=[("tmpc", ti), ("ps", bu)], w=[("act", sl * 2 + j)])
        g.ws.release()
    wo = wview(w_out_b)
    for sl in range(4):
        parts = [(0, (22, 256), wo[:, :, sl * 256:(sl + 1) * 256], wkeys[1])]
        slot = g.ws.get(parts)
        if not g.planning:
            ww = slab(g, slot, 0, 22, 256)
            for j in range(2):
                m = sl * 2 + j
                bk = g.psr.next()
                for k in range(22):
                    mm(g, bk, (0, TT), ww[:, k, j * 128:(j + 1) * 128], g.act[:, k, :], k == 0, k == 21,
                       r=[("ring", slot, 0), ("act", k)])
                pv = g.ps[bk][:, :]
                xin = xb[:, m, xo:xo + TT]
                s.op("dve", lambda h, xin=xin, pv=pv, m=m: h.scalar_tensor_tensor(
                    out=xin, in0=pv, scalar=gate[:, m:m + 1], in1=xin, op0=ALU.mult, op1=ALU.add),
                    r=[("ps", bk), ("x", m), "ada"], w=[("x", m)])
        g.ws.release()


def cast_weight(g, dst, src, key, nsplit=1):
    rows = src.shape[0]
    step = rows // nsplit
    for i in range(nsplit):
        d = dst[i * step:(i + 1) * step, :]
        sr = src[i * step:(i + 1) * step, :]
        g.s.op("pool", lambda h, d=d, sr=sr: h.dma_start(out=d, in_=sr), w=[key], dsem=("cast", key))


def alloc_common(nc, es, g, xw):
    sb = lambda name, shape, dt: es.enter_context(nc.sbuf_tensor(name, shape, dt))
    g.ps = [es.enter_context(nc.psum_tensor("ps%d" % i, [128, 512], F32)) for i in range(8)]
    g.xb = sb("xb", [128, 8, xw], F32)
    g.hb = sb("hb", [128, 8, xw], BF16)
    g.act = sb("act", [128, 22, TT], BF16)
    g.ring = sb("ring", [128, RING, SLOT], BF16)
    g.adaw = sb("adaw", [128, 2, 8, 128], F32)
    g.sq = sb("sq", [128, 2, TT], BF16)
    g.tmpa = sb("tmpa", [128, TT], F32)
    g.rstd = sb("rstd", [128, TT], F32)
    g.tmpb = sb("tmpb", [128, 2, TT], F32)
    g.tmpc = sb("tmpc", [128, 2, TT], F32)
    g.ones_bf = sb("ones_bf", [128, 128], BF16)
    g.epsc = sb("epsc", [128, 1], F32)
    g.cact = sb("cact", [128, 8], F32)
    g.cv = sb("cv", [128, 8], F32)
    g.flg = sb("flg", [128, 16], F32)
    g.sqi = g.tbi = g.adai = g.tci = g.sti = g.pti = 0
    g.psr = PS(g, range(8))


L0V = dict(mix_g=0, mix_b=8, ffn_g=32, ffn_b=40, kv_g=64, kv_b=72, cb_in=88, cb_dw=104, ln_g=112, ln_b=120,
           cb_out=128, w_dw=136, fb=136 + 8 * KW)
L0NV = 136 + 8 * KW + 64


def build_l0(debug=False):
    import contextlib
    nc = bass.Bass("TRN2", target_bir_lowering=False)
    dt_in = lambda n, sh, dt=F32: nc.dram_tensor(n, sh, dt, kind="ExternalInput").ap()
    xin = dt_in("xin", [NT, 8, 128, TT + HALO])
    cvec = dt_in("cvec", [128, 8])
    flags = dt_in("flags", [128, 16])
    flagt = dt_in("flagt", [128, 13, 128])
    vecs = dt_in("vecs", [128, L0NV])
    mix_ada_w = dt_in("mix_ada_w", [D, 3 * D])
    ffn_ada_w = dt_in("ffn_ada_w", [D, 3 * D])
    kv_ada_w = dt_in("kv_ada_w", [D, 2 * D])
    conv_w_in = dt_in("conv_w_in", [D, 2 * D])
    conv_w_out = dt_in("conv_w_out", [D, D])
    ffn_w_in = dt_in("ffn_w_in", [D, 2 * F])
    ffn_w_out = dt_in("ffn_w_out", [F, D])
    kv_w = dt_in("kv_w", [D, 2 * D + NH])
    x1 = nc.dram_tensor("x1", [NT, 8, 128, TT], F32, kind="ExternalOutput").ap()
    KT = nc.dram_tensor("KT", [D, NT * TT], BF16, kind="ExternalOutput").ap()
    V = nc.dram_tensor("V", [NT * TT, D], BF16, kind="ExternalOutput").ap()
    LF = nc.dram_tensor("LF", [NT * TT, NH], F32, kind="ExternalOutput").ap()
    if debug:
        dbg_h = nc.dram_tensor("dbg_h", [128, 8, TT + HALO], BF16, kind="ExternalOutput").ap()
        dbg_u = nc.dram_tensor("dbg_u", [8, 128, TT + HALO], F32, kind="ExternalOutput").ap()
    bw = lambda n, sh: nc.dram_tensor(n, sh, BF16, kind="Internal").ap()
    b_cin = bw("b_cin", [D, 2 * D])
    b_cout = bw("b_cout", [D, D])
    b_fin = bw("b_fin", [D, 2 * F])
    b_fout = bw("b_fout", [F, D])
    b_kv = bw("b_kv", [D, 2 * D + NH])

    g = Ctx()
    with contextlib.ExitStack() as es:
        alloc_common(nc, es, g, TT + HALO)
        sb = lambda name, shape, dt: es.enter_context(nc.sbuf_tensor(name, shape, dt))
        g.vecs = sb("vecs_sb", [128, L0NV], F32)
        g.ada_mix = sb("ada_mix", [128, 24], F32)
        g.ada_ffn = sb("ada_ffn", [128, 24], F32)
        g.ada_kv = sb("ada_kv", [128, 16], F32)
        g.gsv = sb("gsv", [128, 3, 8], F32)
        g.gbv = sb("gbv", [128, 8], F32)
        g.ub = sb("ub", [128, 2, TT + HALO], F32)
        g.acc = sb("acc", [128, 8, TT], F32)
        g.vbs = sb("vbs", [128, 2, 2, TT], BF16)
        g.mean = sb("mean", [128, TT], F32)
        g.ktb = sb("ktb", [128, 2, TT], BF16)
        g.vt = sb("vt", [128, 4, D], BF16)
        g.lft = sb("lft", [128, 4, NH], F32)
        g.zf = sb("zf", [128, 64], F32)
        g.flgt = sb("flgt", [128, 128], F32)

        for planning in (True, False):
            g.planning = planning
            if planning:
                g.s = Sched(nc)
                g.ws = WStream(g)
            else:
                plan = g.ws.plan
                g.s = Sched(nc)
                g.ws = WStream(g)
                g.ws.plan = plan
                g.sqi = g.tbi = g.adai = g.tci = g.sti = g.pti = 0
                g.psr = PS(g, range(8))
            s = g.s
            vv = g.vecs
            s.op("pool", lambda h: h.memset(g.ones_bf[:, :], 1.0), w=["const"])
            s.op("pool", lambda h: h.memset(g.epsc[:, :], EPS), w=["const"])
            s.op("sp", lambda h: h.dma_start(out=g.vecs[:, :], in_=vecs[:, :]), w=["vecs"], dsem="vecs")
            s.op("sp", lambda h: h.dma_start(out=g.cv[:, :], in_=cvec[:, :]), w=["cv"], dsem="cv")
            s.op("sp", lambda h: h.dma_start(out=g.flg[:, :], in_=flags[:, :]), w=["flg"], dsem="flg")
            s.op("sp", lambda h: h.dma_start(out=g.flgt[:, :], in_=flagt[:, 0, :]), w=["flgt"], dsem="flgt")
            cast_weight(g, b_cin, conv_w_in, "w_cin", 2)
            cast_weight(g, b_cout, conv_w_out, "w_cout")
            cast_weight(g, b_fin, ffn_w_in, "w_fin", 4)
            cast_weight(g, b_fout, ffn_w_out, "w_fout", 2)
            cast_weight(g, b_kv, kv_w, "w_kv", 2)
            s.op("act", lambda h: h.activation(out=g.cact[:, :], in_=g.cv[:, :], func=AF.Silu), r=["cv"], w=["cact"], short=True)
            ada_compute(g, mix_ada_w, 24, vv[:, L0V["mix_b"]:L0V["mix_b"] + 24], g.ada_mix)
            ada_compute(g, ffn_ada_w, 24, vv[:, L0V["ffn_b"]:L0V["ffn_b"] + 24], g.ada_ffn)
            ada_compute(g, kv_ada_w, 16, vv[:, L0V["kv_b"]:L0V["kv_b"] + 16], g.ada_kv)
            for i, (ad, gk) in enumerate(((g.ada_mix, "mix_g"), (g.ada_ffn, "ffn_g"), (g.ada_kv, "kv_g"))):
                s.op("dve", lambda h, i=i, ad=ad, gk=gk: h.scalar_tensor_tensor(
                    out=g.gsv[:, i, :], in0=ad[:, 8:16], scalar=1.0, in1=vv[:, L0V[gk]:L0V[gk] + 8],
                    op0=ALU.add, op1=ALU.mult), r=["ada", "vecs"], w=["ada"], short=True)
            s.op("dve", lambda h: h.tensor_tensor(out=g.gbv[:, :], in0=g.ada_mix[:, 16:24],
                                                  in1=vv[:, L0V["cb_out"]:L0V["cb_out"] + 8], op=ALU.mult),
                 r=["ada", "vecs"], w=["ada"], short=True)
            if not planning:
                g.ws.start()

            wci = wview(b_cin)
            wco = wview(b_cout)
            wkv = wview(b_kv)
            for t in range(NT):
                xb = g.xb
                src = xin[t].rearrange("c p n -> p c n")
                s.op("sp", lambda h, src=src: h.dma_start(out=xb[:, :, :], in_=src),
                     w=[("x", c) for c in range(8)], dsem="xin")
                norm_mod(g, xb, [(0, HALO), (HALO, HALO + TT)], g.gsv[:, 0, :], g.ada_mix[:, 0:8], g.hb, 0)
                if debug and t == 0:
                    s.op("sp", lambda h: h.dma_start(out=dbg_h[:, :, :], in_=g.hb[:, :, :]),
                         r=[("h", c) for c in range(8)], dsem="dbgh")
                bs1 = 6
                bs2 = 7
                g.psr = PS(g, range(6))
                for sl in range(4):
                    parts = [(0, (8, 256), wci[:, :, sl * 256:(sl + 1) * 256], "w_cin"),
                             (2048, (8, 256), wci[:, :, D + sl * 256:D + (sl + 1) * 256], "w_cin")]
                    slot = g.ws.get(parts)
                    if not planning:
                        wa = slab(g, slot, 0, 8, 256)
                        wg = slab(g, slot, 2048, 8, 256)
                        for j in range(2):
                            c = sl * 2 + j
                            ba = g.psr.next()
                            bg = g.psr.next()
                            bh = g.psr.next()
                            for k in range(8):
                                mm(g, ba, (0, TT), wa[:, k, j * 128:(j + 1) * 128], g.hb[:, k, HALO:HALO + TT],
                                   k == 0, k == 7, r=[("ring", slot, 0), ("h", k)])
                            for k in range(8):
                                mm(g, bg, (0, TT), wg[:, k, j * 128:(j + 1) * 128], g.hb[:, k, HALO:HALO + TT],
                                   k == 0, k == 7, r=[("ring", slot, 1), ("h", k)])
                            for k in range(8):
                                mm(g, bh, (0, HALO), wa[:, k, j * 128:(j + 1) * 128], g.hb[:, k, 0:HALO],
                                   k == 0, k == 7, r=[("ring", slot, 0), ("h", k)])
                            for k in range(8):
                                mm(g, bh, (HALO, 2 * HALO), wg[:, k, j * 128:(j + 1) * 128], g.hb[:, k, 0:HALO],
                                   k == 0, k == 7, r=[("ring", slot, 1), ("h", k)])
                            ui = c % 2
                            u = g.ub[:, ui, :]
                            ti = g.tci % 2
                            g.tci += 1
                            sg = g.tmpc[:, ti, :]
                            b_a = vv[:, L0V["cb_in"] + c:L0V["cb_in"] + c + 1]
                            b_g = vv[:, L0V["cb_in"] + 8 + c:L0V["cb_in"] + 8 + c + 1]
                            pa = g.ps[ba][:, :]
                            pg = g.ps[bg][:, :]
                            ph = g.ps[bh]
                            s.op("act", lambda h, sg=sg, pg=pg, b_g=b_g: h.activation(out=sg, in_=pg, func=AF.Sigmoid, bias=b_g),
                                 r=[("ps", bg), "vecs"], w=[("tmpc", ti)])
                            s.op("dve", lambda h, u=u, pa=pa, b_a=b_a, sg=sg: h.scalar_tensor_tensor(
                                out=u[:, HALO:HALO + TT], in0=pa, scalar=b_a, in1=sg, op0=ALU.add, op1=ALU.mult),
                                r=[("ps", ba), ("tmpc", ti), "vecs"], w=[("u", ui)])
                            sgh = g.tmpa[:, 0:HALO]
                            s.op("act", lambda h, sgh=sgh, ph=ph, b_g=b_g: h.activation(
                                out=sgh, in_=ph[:, HALO:2 * HALO], func=AF.Sigmoid, bias=b_g),
                                r=[("ps", bh), "vecs"], w=["tmpa"], short=True)
                            s.op("dve", lambda h, u=u, ph=ph, b_a=b_a, sgh=sgh: h.scalar_tensor_tensor(
                                out=u[:, 0:HALO], in0=ph[:, 0:HALO], scalar=b_a, in1=sgh, op0=ALU.add, op1=ALU.mult),
                                r=[("ps", bh), "tmpa", "vecs"], w=[("u", ui)], short=True)
                            if t == 0:
                                s.op("dve", lambda h, u=u: h.tensor_tensor(
                                    out=u[:, 0:HALO], in0=u[:, 0:HALO], in1=g.flgt[:, 0:HALO], op=ALU.mult),
                                    r=[("u", ui), "flgt"], w=[("u", ui)], short=True)
                            if debug and t == 0:
                                s.op("sp", lambda h, u=u, c=c: h.dma_start(out=dbg_u[c, :, :], in_=u),
                                     r=[("u", ui)], dsem=("dbgu", ui))
                            acc = g.acc[:, c, :]
                            wd = L0V["w_dw"] + c * KW
                            bdw = vv[:, L0V["cb_dw"] + c:L0V["cb_dw"] + c + 1]
                            s.op("dve", lambda h, acc=acc, u=u, wd=wd, bdw=bdw: h.tensor_scalar(
                                out=acc, in0=u[:, 2:2 + TT], scalar1=vv[:, wd:wd + 1], scalar2=bdw, op0=ALU.mult, op1=ALU.add),
                                r=[("u", ui), "vecs"], w=[("acc", c)])
                            for jj in range(1, KW):
                                s.op("dve", lambda h, acc=acc, u=u, wd=wd, jj=jj: h.scalar_tensor_tensor(
                                    out=acc, in0=u[:, 2 + jj:2 + jj + TT], scalar=vv[:, wd + jj:wd + jj + 1], in1=acc,
                                    op0=ALU.mult, op1=ALU.add), r=[("u", ui), "vecs", ("acc", c)], w=[("acc", c)])
                            vi = c % 2
                            vb = g.vbs[:, vi, 0, :]
                            vq = g.vbs[:, vi, 1, :]
                            s.op("act", lambda h, vb=vb, acc=acc: h.activation(out=vb, in_=acc, func=AF.Identity),
                                 r=[("acc", c)], w=[("vb", vi)])
                            s.op("act", lambda h, vq=vq, acc=acc: h.activation(out=vq, in_=acc, func=AF.Square),
                                 r=[("acc", c)], w=[("vq", vi)])
                            mm(g, bs1, (0, TT), g.ones_bf[:, :], vb, c == 0, c == 7, r=[("vb", vi), "const"])
                            mm(g, bs2, (0, TT), g.ones_bf[:, :], vq, c == 0, c == 7, r=[("vq", vi), "const"])
                    g.ws.release()
                if not planning:
                    p1 = g.ps[bs1][:, :]
                    p2 = g.ps[bs2][:, :]
                    s.op("dve", lambda h, p1=p1: h.tensor_scalar(out=g.mean[:, :], in0=p1, scalar1=1.0 / D, scalar2=None, op0=ALU.mult),
                         r=[("ps", bs1)], w=["mean"])
                    s.op("dve", lambda h: h.tensor_tensor(out=g.tmpa[:, :], in0=g.mean[:, :], in1=g.mean[:, :], op=ALU.mult),
                         r=["mean"], w=["tmpa"])
                    s.op("dve", lambda h, p2=p2: h.scalar_tensor_tensor(
                        out=g.tmpa[:, :], in0=p2, scalar=1.0 / D, in1=g.tmpa[:, :], op0=ALU.mult, op1=ALU.subtract),
                        r=[("ps", bs2), "tmpa"], w=["tmpa"])
                    s.op("act", lambda h: h.activation(out=g.tmpa[:, :], in_=g.tmpa[:, :], func=AF.Sqrt, bias=g.epsc[:, 0:1], scale=1.0),
                         r=["tmpa", "const"], w=["tmpa"])
                    s.op("dve", lambda h: h.reciprocal(out=g.rstd[:, :], in_=g.tmpa[:, :]), r=["tmpa"], w=["rstd"])
                    for c in range(8):
                        acc = g.acc[:, c, :]
                        s.op("dve", lambda h, acc=acc: h.tensor_tensor(out=acc, in0=acc, in1=g.mean[:, :], op=ALU.subtract),
                             r=[("acc", c), "mean"], w=[("acc", c)])
                        s.op("dve", lambda h, acc=acc: h.tensor_tensor(out=acc, in0=acc, in1=g.rstd[:, :], op=ALU.mult),
                             r=[("acc", c), "rstd"], w=[("acc", c)])
                        ho = g.hb[:, c, 0:TT]
                        s.op("act", lambda h, ho=ho, acc=acc, c=c: h.activation(
                            out=ho, in_=acc, func=AF.Silu, bias=vv[:, L0V["ln_b"] + c:L0V["ln_b"] + c + 1],
                            scale=vv[:, L0V["ln_g"] + c:L0V["ln_g"] + c + 1]), r=[("acc", c), "vecs"], w=[("h", c)])
                g.psr = PS(g, range(8))
                for sl in range(2):
                    parts = [(0, (8, 512), wco[:, :, sl * 512:(sl + 1) * 512], "w_cout")]
                    slot = g.ws.get(parts)
                    if not planning:
                        ww = slab(g, slot, 0, 8, 512)
                        for j in range(4):
                            m = sl * 4 + j
                            bk = g.psr.next()
                            for k in range(8):
                                mm(g, bk, (0, TT), ww[:, k, j * 128:(j + 1) * 128], g.hb[:, k, 0:TT], k == 0, k == 7,
                                   r=[("ring", slot, 0), ("h", k)])
                            ti = g.tci % 2
                            g.tci += 1
                            tc_ = g.tmpc[:, ti, :]
                            pv = g.ps[bk][:, :]
                            s.op("act", lambda h, tc_=tc_, pv=pv, m=m: h.activation(
                                out=tc_, in_=pv, func=AF.Identity, bias=g.gbv[:, m:m + 1], scale=g.ada_mix[:, 16 + m:17 + m]),
                                r=[("ps", bk), "ada"], w=[("tmpc", ti)])
                            xm = xb[:, m, HALO:HALO + TT]
                            s.op("dve", lambda h, xm=xm, tc_=tc_: h.tensor_tensor(out=xm, in0=xm, in1=tc_, op=ALU.add),
                                 r=[("x", m), ("tmpc", ti)], w=[("x", m)])
                    g.ws.release()
                norm_mod(g, xb, [(HALO, HALO + TT)], g.gsv[:, 1, :], g.ada_ffn[:, 0:8], g.hb, HALO)
                ffn(g, xb, HALO, b_fin, b_fout, g.ada_ffn[:, 16:24], ("w_fin", "w_fout"))
                dst = x1[t].rearrange("c p n -> p c n")
                s.op("sp", lambda h, dst=dst: h.dma_start(out=dst, in_=xb[:, :, HALO:HALO + TT]),
                     r=[("x", c) for c in range(8)], dsem="x1")
                norm_mod(g, xb, [(HALO, HALO + TT)], g.gsv[:, 2, :], g.ada_kv[:, 0:8], g.hb, HALO)
                for sl in range(2):
                    parts = [(0, (8, 512), wkv[:, :, sl * 512:(sl + 1) * 512], "w_kv")]
                    slot = g.ws.get(parts)
                    if not planning:
                        ww = slab(g, slot, 0, 8, 512)
                        for j in range(4):
                            m = sl * 4 + j
                            bk = g.psr.next()
                            for k in range(8):
                                mm(g, bk, (0, TT), ww[:, k, j * 128:(j + 1) * 128], g.hb[:, k, 0:TT], k == 0, k == 7,
                                   r=[("ring", slot, 0), ("h", k)])
                            ki = m % 2
                            kb_ = g.ktb[:, ki, :]
                            pv = g.ps[bk][:, :]
                            s.op("act", lambda h, kb_=kb_, pv=pv: h.activation(out=kb_, in_=pv, func=AF.Identity),
                                 r=[("ps", bk)], w=[("ktb", ki)])
                            dst = KT[m * 128:(m + 1) * 128, t * TT:(t + 1) * TT]
                            s.op("sp", lambda h, dst=dst, kb_=kb_: h.dma_start(out=dst, in_=kb_),
                                 r=[("ktb", ki)], dsem=("kt", ki))
                    g.ws.release()
                for sl in range(2):
                    parts = [(0, (8, 512), wkv[:, :, D + sl * 512:D + (sl + 1) * 512], "w_kv")]
                    slot = g.ws.get(parts)
                    if not planning:
                        ww = slab(g, slot, 0, 8, 512)
                        for tb in range(4):
                            bk = g.psr.next()
                            for k in range(8):
                                mm(g, bk, (0, 512), g.hb[:, k, tb * 128:(tb + 1) * 128], ww[:, k, :], k == 0, k == 7,
                                   r=[("ring", slot, 0), ("h", k)])
                            pv = g.ps[bk][:, :]
                            vo = g.vt[:, tb, sl * 512:(sl + 1) * 512]
                            if tb % 2 == 0:
                                s.op("act", lambda h, vo=vo, pv=pv: h.activation(out=vo, in_=pv, func=AF.Identity),
                                     r=[("ps", bk)], w=[("vt", tb, sl)])
                            else:
                                s.op("dve", lambda h, vo=vo, pv=pv: h.tensor_copy(out=vo, in_=pv),
                                     r=[("ps", bk)], w=[("vt", tb, sl)])
                    g.ws.release()
                dst = V[t * TT:(t + 1) * TT, :].rearrange("(tb p) m -> p tb m", p=128)
                s.op("sp", lambda h, dst=dst: h.dma_start(out=dst, in_=g.vt[:, :, :]),
                     r=[("vt", tb, sl) for tb in range(4) for sl in range(2)], dsem="vst")
                parts = [(0, (8, NH), wkv[:, :, 2 * D:2 * D + NH], "w_kv")]
                slot = g.ws.get(parts)
                if not planning:
                    ww = slab(g, slot, 0, 8, NH)
                    bk = g.psr.next()
                    for tb in range(4):
                        for k in range(8):
                            mm(g, bk, (tb * NH, (tb + 1) * NH), g.hb[:, k, tb * 128:(tb + 1) * 128], ww[:, k, :],
                               k == 0, k == 7, r=[("ring", slot, 0), ("h", k)])
                    pv = g.ps[bk][:, 0:64]
                    s.op("dve", lambda h, pv=pv: h.tensor_tensor(out=g.zf[:, :], in0=pv, in1=vv[:, L0V["fb"]:L0V["fb"] + 64], op=ALU.add),
                         r=[("ps", bk), "vecs"], w=["zf"], short=True)
                    s.op("act", lambda h: h.activation(out=g.zf[:, :], in_=g.zf[:, :], func=AF.Exp, scale=-1.0), r=["zf"], w=["zf"], short=True)
                    s.op("act", lambda h: h.activation(out=g.zf[:, :], in_=g.zf[:, :], func=AF.Ln, bias=1.0), r=["zf"], w=["zf"], short=True)
                    lfv = g.lft[:, :, :].rearrange("p a b -> p (a b)")
                    s.op("dve", lambda h, lfv=lfv: h.tensor_scalar(out=lfv, in0=g.zf[:, :], scalar1=-1.0, scalar2=None, op0=ALU.mult),
                         r=["zf"], w=["lft"], short=True)
                    dst = LF[t * TT:(t + 1) * TT, :].rearrange("(tb p) m -> p tb m", p=128)
                    s.op("sp", lambda h, dst=dst: h.dma_start(out=dst, in_=g.lft[:, :, :]), r=["lft"], dsem="lfst")
                g.ws.release()
        g.s.emit(final_waits=["x1", ("kt", 0), ("kt", 1), "vst", "lfst"])
    return nc


L1V = dict(mix_g=0, mix_b=8, ffn_g=32, ffn_b=40, fin_g=64)
L1NV = 72


def build_l1():
    import contextlib
    nc = bass.Bass("TRN2", target_bir_lowering=False)
    dt_in = lambda n, sh, dt=F32: nc.dram_tensor(n, sh, dt, kind="ExternalInput").ap()
    x1 = dt_in("x1", [NT, 8, 128, TT])
    cvec = dt_in("cvec", [128, 8])
    flags = dt_in("flags", [128, 16])
    flagt = dt_in("flagt", [128, 13, 128])
    vecs = dt_in("vecs", [128, L1NV])
    consts = dt_in("consts", [128, 4, 128])
    KTo = dt_in("KTo", [D, NT * TT], BF16)
    Vo = dt_in("Vo", [NT * TT, D], BF16)
    KTg = dt_in("KTg", [2, D, NT * TT], BF16)
    Vg = dt_in("Vg", [2, NT * TT, D], BF16)
    LFg = dt_in("LFg", [2, NT * TT, NH])
    mix_ada_w = dt_in("mix_ada_w", [D, 3 * D])
    ffn_ada_w = dt_in("ffn_ada_w", [D, 3 * D])
    attn_w_q = dt_in("attn_w_q", [D, D])
    attn_w_o = dt_in("attn_w_o", [D, D])
    ffn_w_in = dt_in("ffn_w_in", [D, 2 * F])
    ffn_w_out = dt_in("ffn_w_out", [F, D])
    out = nc.dram_tensor("out", [NT, 8, 128, TT], F32, kind="ExternalOutput").ap()
    bw = lambda n, sh: nc.dram_tensor(n, sh, BF16, kind="Internal").ap()
    b_q = bw("b_q", [D, D])
    b_o = bw("b_o", [D, D])
    b_fin = bw("b_fin", [D, 2 * F])
    b_fout = bw("b_fout", [F, D])

    g = Ctx()
    with contextlib.ExitStack() as es:
        alloc_common(nc, es, g, TT)
        sb = lambda name, shape, dt: es.enter_context(nc.sbuf_tensor(name, shape, dt))
        g.vecs = sb("vecs_sb", [128, L1NV], F32)
        g.ada_mix = sb("ada_mix", [128, 24], F32)
        g.ada_ffn = sb("ada_ffn", [128, 24], F32)
        g.gsv = sb("gsv", [128, 2, 8], F32)
        g.cst = sb("cst", [128, 4, 128], F32)
        g.tri_bf = sb("tri_bf", [128, 128], BF16)
        g.qt = sb("qt", [128, 8, TT], BF16)
        g.ot = sb("ot", [128, 8, TT], BF16)
        g.kts = sb("kts", [128, 3, 1024], BF16)
        g.vs = sb("vs", [128, 3, 8, 192], BF16)
        g.pt = sb("pt", [128, 4, TT], BF16)
        g.T1 = sb("T1", [128, 64, NH], F32)
        g.T2 = sb("T2", [128, 64, NH], F32)
        g.T3 = sb("T3", [128, 64, NH], F32)
        g.onesf = sb("onesf", [128, 64], F32)
        g.own = sb("own", [128, 8, NH], F32)
        g.crefb = sb("crefb", [128, NH], F32)
        g.biasF = sb("biasF", [128, 56, NH], F32)
        g.biasO = sb("biasO", [128, 8, NH], F32)
        g.gd = sb("gd", [128, 4, NH], F32)
        g.Rp = sb("Rp", [128, 2, 2, TT], BF16)
        g.flgt = sb("flgt", [128, 13, 8, NH], F32)
        g.owt = sb("owt", [128, 8, NH], F32)
        g.rcp = sb("rcp", [128, 2, TT], F32)

        for planning in (True, False):
            g.planning = planning
            if planning:
                g.s = Sched(nc)
                g.ws = WStream(g)
            else:
                plan = g.ws.plan
                g.s = Sched(nc)
                g.ws = WStream(g)
                g.ws.plan = plan
                g.sqi = g.tbi = g.adai = g.tci = g.sti = g.pti = 0
                g.psr = PS(g, range(8))
            s = g.s
            vv = g.vecs
            s.op("pool", lambda h: h.memset(g.ones_bf[:, :], 1.0), w=["const"])
            s.op("pool", lambda h: h.memset(g.epsc[:, :], EPS), w=["const"])
            s.op("pool", lambda h: h.memset(g.onesf[:, :], 1.0), w=["const"])
            s.op("pool", lambda h: h.memset(g.vs[:, :, :, 64:128], 1.0), w=["vones"])
            s.op("sp", lambda h: h.dma_start(out=g.vecs[:, :], in_=vecs[:, :]), w=["vecs"], dsem="vecs")
            s.op("sp", lambda h: h.dma_start(out=g.cv[:, :], in_=cvec[:, :]), w=["cv"], dsem="cv")
            s.op("sp", lambda h: h.dma_start(out=g.flg[:, :], in_=flags[:, :]), w=["flg"], dsem="flg")
            s.op("sp", lambda h: h.dma_start(out=g.cst[:, :, :], in_=consts[:, :, :]), w=["cst"], dsem="cst")
            s.op("sp", lambda h: h.dma_start(out=g.flgt[:, :, :, :].rearrange("p a b c -> p a (b c)"), in_=flagt[:, :, :]), w=["flgt"], dsem="flgt")
            cast_weight(g, b_q, attn_w_q, "w_q")
            cast_weight(g, b_o, attn_w_o, "w_o")
            cast_weight(g, b_fin, ffn_w_in, "w_fin", 4)
            cast_weight(g, b_fout, ffn_w_out, "w_fout", 2)
            s.op("dve", lambda h: h.tensor_copy(out=g.tri_bf[:, :], in_=g.cst[:, 0, :]), r=["cst"], w=["tribf"], short=True)
            s.op("act", lambda h: h.activation(out=g.cact[:, :], in_=g.cv[:, :], func=AF.Silu), r=["cv"], w=["cact"], short=True)
            ada_compute(g, mix_ada_w, 24, vv[:, L1V["mix_b"]:L1V["mix_b"] + 24], g.ada_mix)
            ada_compute(g, ffn_ada_w, 24, vv[:, L1V["ffn_b"]:L1V["ffn_b"] + 24], g.ada_ffn)
            for i, (ad, gk) in enumerate(((g.ada_mix, "mix_g"), (g.ada_ffn, "ffn_g"))):
                s.op("dve", lambda h, i=i, ad=ad, gk=gk: h.scalar_tensor_tensor(
                    out=g.gsv[:, i, :], in0=ad[:, 8:16], scalar=1.0, in1=vv[:, L1V[gk]:L1V[gk] + 8],
                    op0=ALU.add, op1=ALU.mult), r=["ada", "vecs"], w=["ada"], short=True)
            for j in range(8):
                rk, lc = GCH[j]
                src = LFg[rk, lc * 1024:(lc + 1) * 1024, :].rearrange("(kb p) h -> p kb h", p=128)
                s.op("sp", lambda h, src=src, j=j: h.dma_start(out=g.T1[:, j * 8:(j + 1) * 8, :], in_=src),
                     w=[("lfin", j)], dsem="lfin")
            lff = g.T1[:, :, :].rearrange("p a b -> p (a b)")
            for hf in range(2):
                b1 = g.psr.next()
                mm(g, b1, (0, 512), g.cst[:, 0, :], lff[:, hf * 512:(hf + 1) * 512], True, True,
                   r=[("lfin", j) for j in range(8)] + ["cst"])
                w2 = g.T2[:, :, :].rearrange("p a b -> p (a b)")[:, hf * 512:(hf + 1) * 512]
                pv = g.ps[b1][:, :]
                s.op("dve", lambda h, w2=w2, pv=pv: h.tensor_copy(out=w2, in_=pv), r=[("ps", b1)], w=[("T2", hf)])
                b2 = g.psr.next()
                mm(g, b2, (0, 512), g.cst[:, 1, :], lff[:, hf * 512:(hf + 1) * 512], True, True,
                   r=[("lfin", j) for j in range(8)] + ["cst"])
                w3 = g.T3[:, :, :].rearrange("p a b -> p (a b)")[:, hf * 512:(hf + 1) * 512]
                pv2 = g.ps[b2][:, :]
                s.op("act", lambda h, w3=w3, pv2=pv2: h.activation(out=w3, in_=pv2, func=AF.Identity), r=[("ps", b2)], w=[("T3", hf)])
            for hh in range(NH):
                s.op("dve", lambda h, hh=hh: h.tensor_tensor_scan(
                    out=g.T1[:, :, hh], data0=g.onesf[:, :], data1=g.T3[:, :, hh], initial=0.0, op0=ALU.mult, op1=ALU.add),
                    r=[("T3", 0), ("T3", 1), "const"] + [("lfin", j) for j in range(8)], w=[("lfin", j) for j in range(8)], short=True)
            s.op("dve", lambda h: h.tensor_tensor(out=g.T1[:, :, :], in0=g.T1[:, :, :], in1=g.T3[:, :, :], op=ALU.subtract),
                 r=[("T3", 0), ("T3", 1)] + [("lfin", j) for j in range(8)], w=["CG"])
            s.op("dve", lambda h: h.tensor_tensor(out=g.T1[:, :, :], in0=g.T1[:, :, :], in1=g.T2[:, :, :], op=ALU.add),
                 r=[("T2", 0), ("T2", 1), "CG"], w=["CG"])
            if not planning:
                g.ws.start()
            wq = wview(b_q)
            wo_ = wview(b_o)
            kvn = 0
            for t in range(NT):
                i, sx = t // 2, t % 2
                mi = MI[i]
                xb = g.xb
                src = x1[t].rearrange("c p n -> p c n")
                s.op("sp", lambda h, src=src: h.dma_start(out=xb[:, :, :], in_=src),
                     w=[("x", c) for c in range(8)], dsem="xin")
                g.psr = PS(g, range(8))
                norm_mod(g, xb, [(0, TT)], g.gsv[:, 0, :], g.ada_mix[:, 0:8], g.hb, 0)
                for sl in range(2):
                    parts = [(0, (8, 512), wq[:, :, sl * 512:(sl + 1) * 512], "w_q")]
                    slot = g.ws.get(parts)
                    if not planning:
                        ww = slab(g, slot, 0, 8, 512)
                        for j in range(4):
                            m = sl * 4 + j
                            bk = g.psr.next()
                            for k in range(8):
                                mm(g, bk, (0, TT), ww[:, k, j * 128:(j + 1) * 128], g.hb[:, k, 0:TT], k == 0, k == 7,
                                   r=[("ring", slot, 0), ("h", k)])
                            pv = g.ps[bk][:, :]
                            qo = g.qt[:, m, :]
                            s.op("act", lambda h, qo=qo, pv=pv: h.activation(out=qo, in_=pv, func=AF.Identity, scale=0.125),
                                 r=[("ps", bk)], w=[("qt", m)])
                    g.ws.release()
                if not planning:
                    if sx == 0:
                        lo_ = g.T1[:, 8 * (mi - 1):8 * mi, :]
                        hi_ = g.T1[:, 8 * mi:8 * mi + 8, :]
                        s.op("dve", lambda h, hi_=hi_, i=i: h.tensor_tensor(
                            out=g.own[:, :, :], in0=hi_, in1=g.flgt[:, 5 + i, :, :], op=ALU.mult),
                            r=["CG", "flgt"], w=["own"], short=True)
                        s.op("dve", lambda h, lo_=lo_, i=i: h.tensor_tensor(
                            out=g.owt[:, :, :], in0=lo_, in1=g.flgt[:, 1 + i, :, :], op=ALU.mult),
                            r=["CG", "flgt"], w=["owt"], short=True)
                        s.op("dve", lambda h: h.tensor_tensor(
                            out=g.own[:, :, :], in0=g.own[:, :, :], in1=g.owt[:, :, :], op=ALU.add),
                            r=["own", "owt"], w=["own"], short=True)
                    bc = g.psr.next()
                    mm(g, bc, (0, NH), g.cst[:, 2, :], g.own[:, 4 * sx + 2, :], True, True, r=["own", "cst"])
                    pvc = g.ps[bc][:, 0:NH]
                    s.op("dve", lambda h, pvc=pvc: h.tensor_copy(out=g.crefb[:, :], in_=pvc), r=[("ps", bc)], w=["crefb"], short=True)
                    nfb = 8 * mi
                    s.op("dve", lambda h, nfb=nfb: h.tensor_tensor(
                        out=g.biasF[:, 0:nfb, :], in0=g.crefb[:, :].unsqueeze(1).to_broadcast([128, nfb, NH]),
                        in1=g.T1[:, 0:nfb, :], op=ALU.subtract), r=["crefb", "CG"], w=["biasF"], short=True)
                    s.op("dve", lambda h, nfb=nfb, i=i: h.tensor_tensor(
                        out=g.biasF[:, nfb - 8:nfb, :], in0=g.biasF[:, nfb - 8:nfb, :], in1=g.flgt[:, 9 + i, :, :],
                        op=ALU.add), r=["biasF", "flgt"], w=["biasF"], short=True)
                    s.op("dve", lambda h: h.tensor_tensor(
                        out=g.biasO[:, :, :], in0=g.crefb[:, :].unsqueeze(1).to_broadcast([128, 8, NH]),
                        in1=g.own[:, :, :], op=ALU.subtract), r=["crefb", "own"], w=["biasO"], short=True)
                    s.op("dve", lambda h, sx=sx: h.tensor_tensor(
                        out=g.gd[:, :, :], in0=g.own[:, 4 * sx:4 * sx + 4, :],
                        in1=g.crefb[:, :].unsqueeze(1).to_broadcast([128, 4, NH]), op=ALU.subtract),
                        r=["crefb", "own"], w=["gd"], short=True)
                    for c in range(8):
                        boA, boB = (4, 5) if c % 2 == 0 else (6, 7)
                        rpi = c % 2
                        for e in range(2):
                            for qb in range(4):
                                ro = g.Rp[:, rpi, e, qb * 128:(qb + 1) * 128]
                                s.op("dve", lambda h, ro=ro, qb=qb, hh=2 * c + e: h.tensor_scalar(
                                    out=ro, in0=g.cst[:, 3, :], scalar1=g.gd[:, qb, hh:hh + 1], scalar2=None, op0=ALU.mult),
                                    r=["gd", "cst"], w=[("Rp", rpi, e)])
                        chunks = [("g", j) for j in range(mi)] + [("o", i)]
                        first = True
                        for (kind, j) in chunks:
                            ks = kvn % 3
                            kvn += 1
                            if kind == "g":
                                rk, lc = GCH[j]
                                ksrc = KTg[rk, c * 128:(c + 1) * 128, lc * 1024:(lc + 1) * 1024]
                                vsrc = Vg[rk, lc * 1024:(lc + 1) * 1024, c * 128:(c + 1) * 128]
                                nb = 8
                            else:
                                ksrc = KTo[c * 128:(c + 1) * 128, j * 1024:(j + 1) * 1024]
                                vsrc = Vo[j * 1024:(j + 1) * 1024, c * 128:(c + 1) * 128]
                                nb = 4 * sx + 4
                            vsr = vsrc.rearrange("(kb p) m -> p kb m", p=128)
                            s.op("sp", lambda h, ks=ks, ksrc=ksrc: h.dma_start(out=g.kts[:, ks, :], in_=ksrc),
                                 w=[("kts", ks)], dsem=("kv", ks))
                            s.op("sp", lambda h, ks=ks, vsr=vsr: h.dma_start(out=g.vs[:, ks, :, 0:64], in_=vsr[:, :, 0:64]),
                                 r=["vones"], w=[("vsa", ks)], dsem=("kv", ks))
                            s.op("sp", lambda h, ks=ks, vsr=vsr: h.dma_start(out=g.vs[:, ks, :, 128:192], in_=vsr[:, :, 64:128]),
                                 r=["vones"], w=[("vsb", ks)], dsem=("kv", ks))
                            for kb in range(nb):
                                diag = (kind == "o") and kb >= 4 * sx
                                qlo = 128 * (kb - 4 * sx) if diag else 0
                                last = (kind == "o") and kb == nb - 1
                                bsA = (2 * (g.sti % 2))
                                bsB = bsA + 1
                                g.sti += 1
                                mm(g, bsA, (qlo, TT), g.kts[0:64, ks, kb * 128:(kb + 1) * 128], g.qt[0:64, c, qlo:TT],
                                   True, False, r=[("kts", ks), ("qt", c)], tp=(0, 0))
                                mm(g, bsB, (qlo, TT), g.kts[64:128, ks, kb * 128:(kb + 1) * 128], g.qt[64:128, c, qlo:TT],
                                   True, False, r=[("kts", ks), ("qt", c)], tp=(64, 0))
                                mm(g, bsA, (qlo, TT), g.ones_bf[:, :], g.Rp[:, rpi, 0, qlo:TT], False, True,
                                   r=[("Rp", rpi, 0), "const"])
                                mm(g, bsB, (qlo, TT), g.ones_bf[:, :], g.Rp[:, rpi, 1, qlo:TT], False, True,
                                   r=[("Rp", rpi, 1), "const"])
                                for e, (bs_, bo_, voff) in enumerate(((bsA, boA, 0), (bsB, boB, 64))):
                                    pi = g.pti % 4
                                    g.pti += 1
                                    hh = 2 * c + e
                                    po = g.pt[:, pi, qlo:TT]
                                    if kind == "g":
                                        bias = g.biasF[:, j * 8 + kb, hh:hh + 1]
                                        bkey = "biasF"
                                    else:
                                        bias = g.biasO[:, kb, hh:hh + 1]
                                        bkey = "biasO"
                                    pin = g.ps[bs_][:, qlo:TT]
                                    s.op("act", lambda h, po=po, pin=pin, bias=bias: h.activation(
                                        out=po, in_=pin, func=AF.Exp, bias=bias, scale=1.0),
                                        r=[("ps", bs_), bkey], w=[("pt", pi)])
                                    if diag:
                                        pd = g.pt[:, pi, qlo:qlo + 128]
                                        s.op("pool", lambda h, pd=pd: h.tensor_tensor(out=pd, in0=pd, in1=g.tri_bf[:, :], op=ALU.mult),
                                             r=[("pt", pi), "tribf"], w=[("pt", pi)])
                                    mm(g, bo_, (qlo, TT), g.vs[:, ks, kb, voff:voff + 128], po, first and kb == 0, last,
                                       r=[("pt", pi), ("vsa", ks), ("vsb", ks), "vones"])
                                first = False
                        ri = c % 2
                        pA = g.ps[boA]
                        pB = g.ps[boB]
                        s.op("dve", lambda h, ri=ri, pA=pA: h.reciprocal(out=g.rcp[64:128, ri, :], in_=pA[64:128, :]),
                             r=[("ps", boA)], w=[("rcp", ri, 1)], short=True)
                        s.op("dve", lambda h, ri=ri, pB=pB: h.reciprocal(out=g.rcp[0:64, ri, :], in_=pB[0:64, :]),
                             r=[("ps", boB)], w=[("rcp", ri, 0)], short=True)
                        s.op("dve", lambda h, ri=ri, pA=pA, c=c: h.tensor_tensor(
                            out=g.ot[0:64, c, :], in0=pA[0:64, :], in1=g.rcp[64:128, ri, :], op=ALU.mult),
                            r=[("ps", boA), ("rcp", ri, 1)], w=[("ot", c, 0)])
                        s.op("dve", lambda h, ri=ri, pB=pB, c=c: h.tensor_tensor(
                            out=g.ot[64:128, c, :], in0=pB[64:128, :], in1=g.rcp[0:64, ri, :], op=ALU.mult),
                            r=[("ps", boB), ("rcp", ri, 0)], w=[("ot", c, 1)])
                for sl in range(2):
                    parts = [(0, (8, 512), wo_[:, :, sl * 512:(sl + 1) * 512], "w_o")]
                    slot = g.ws.get(parts)
                    if not planning:
                        ww = slab(g, slot, 0, 8, 512)
                        for j in range(4):
                            m = sl * 4 + j
                            bk = g.psr.next()
                            for k in range(8):
                                mm(g, bk, (0, TT), ww[:, k, j * 128:(j + 1) * 128], g.ot[:, k, :], k == 0, k == 7,
                                   r=[("ring", slot, 0), ("ot", k, 0), ("ot", k, 1)])
                            pv = g.ps[bk][:, :]
                            xm = xb[:, m, :]
                            s.op("dve", lambda h, xm=xm, pv=pv, m=m: h.scalar_tensor_tensor(
                                out=xm, in0=pv, scalar=g.ada_mix[:, 16 + m:17 + m], in1=xm, op0=ALU.mult, op1=ALU.add),
                                r=[("ps", bk), ("x", m), "ada"], w=[("x", m)])
                    g.ws.release()
                norm_mod(g, xb, [(0, TT)], g.gsv[:, 1, :], g.ada_ffn[:, 0:8], g.hb, 0)
                ffn(g, xb, 0, b_fin, b_fout, g.ada_ffn[:, 16:24], ("w_fin", "w_fout"))
                norm_mod(g, xb, [(0, TT)], None, None, None, 0, final_out=True, gvec=vv[:, L1V["fin_g"]:L1V["fin_g"] + 8])
                dst = out[t].rearrange("c p n -> p c n")
                s.op("sp", lambda h, dst=dst: h.dma_start(out=dst, in_=xb[:, :, :]),
                     r=[("x", c) for c in range(8)], dsem="ost")
        g.s.emit(final_waits=["ost"])
    return nc


def fm(v):
    v = np.asarray(v, np.float32)
    return np.ascontiguousarray(v.reshape(-1, 128).T)


_CACHE = {}


def kernel(x, c, mix_norm_g, mix_ada_w, mix_ada_b, ffn_norm_g, ffn_ada_w, ffn_ada_b,
           ffn_w_in, ffn_w_out, conv_w_in, conv_b_in, conv_w_dw, conv_b_dw, conv_ln_g,
           conv_ln_b, conv_w_out, conv_b_out, kv_norm_g, kv_ada_w, kv_ada_b, kv_w,
           forget_b, attn_w_q, attn_w_o, final_norm_g):
    f32 = lambda a: np.ascontiguousarray(np.asarray(a, np.float32))
    x = f32(x)
    c = f32(c)
    cores = [(b, r) for b in range(B) for r in range(2)]
    v0 = np.zeros((128, L0NV), np.float32)
    put = lambda off, a: v0.__setitem__((slice(None), slice(off, off + a.shape[1])), a)
    put(L0V["mix_g"], fm(mix_norm_g[0])); put(L0V["mix_b"], fm(mix_ada_b[0]))
    put(L0V["ffn_g"], fm(ffn_norm_g[0])); put(L0V["ffn_b"], fm(ffn_ada_b[0]))
    put(L0V["kv_g"], fm(kv_norm_g)); put(L0V["kv_b"], fm(kv_ada_b))
    put(L0V["cb_in"], fm(conv_b_in[0])); put(L0V["cb_dw"], fm(conv_b_dw[0]))
    put(L0V["ln_g"], fm(conv_ln_g[0])); put(L0V["ln_b"], fm(conv_ln_b[0])); put(L0V["cb_out"], fm(conv_b_out[0]))
    wdw = f32(conv_w_dw[0])
    put(L0V["w_dw"], np.ascontiguousarray(wdw.T.reshape(8, 128, KW).transpose(1, 0, 2)).reshape(128, 8 * KW))
    put(L0V["fb"], np.tile(f32(forget_b)[None, :], (128, 4)))
    in1 = []
    flag_list = []
    flagt_list = []
    for (b, r) in cores:
        xin = np.zeros((NT, 8, 128, TT + HALO), np.float32)
        for t in range(NT):
            st = CH[r][t // 2] * 1024 + (t % 2) * TT
            lo = max(st - HALO, 0)
            seg = x[b, lo:st + TT, :]
            xin[t, :, :, (HALO + TT) - seg.shape[0]:] = seg.T.reshape(8, 128, seg.shape[0])
        fl = np.zeros((128, 16), np.float32)
        fl[:, 0] = 0.0 if r == 0 else 1.0
        for i in range(4):
            f = 1.0 if CH[r][i] == MI[i] - 1 else 0.0
            fl[:, 1 + i] = f
            fl[:, 5 + i] = 1.0 - f
            fl[:, 9 + i] = -30000.0 * f
        flag_list.append(fl)
        ft = np.ascontiguousarray(np.repeat(fl[:, :13, None], 128, axis=2))
        flagt_list.append(ft)
        in1.append(dict(xin=xin, cvec=fm(c[b]), flags=fl, flagt=ft, vecs=v0,
                        mix_ada_w=f32(mix_ada_w[0]), ffn_ada_w=f32(ffn_ada_w[0]), kv_ada_w=f32(kv_ada_w),
                        conv_w_in=f32(conv_w_in[0]), conv_w_out=f32(conv_w_out[0]),
                        ffn_w_in=f32(ffn_w_in[0]), ffn_w_out=f32(ffn_w_out[0]), kv_w=f32(kv_w)))
    if "l0" not in _CACHE:
        _CACHE["l0"] = build_l0()
    res1 = run_bass_kernel_spmd(_CACHE["l0"], in1, core_ids=list(range(8))).results
    v1 = np.zeros((128, L1NV), np.float32)
    put1 = lambda off, a: v1.__setitem__((slice(None), slice(off, off + a.shape[1])), a)
    put1(L1V["mix_g"], fm(mix_norm_g[1])); put1(L1V["mix_b"], fm(mix_ada_b[1]))
    put1(L1V["ffn_g"], fm(ffn_norm_g[1])); put1(L1V["ffn_b"], fm(ffn_ada_b[1])); put1(L1V["fin_g"], fm(final_norm_g))
    kk = np.arange(128)
    cst = np.zeros((128, 4, 128), np.float32)
    cst[:, 0, :] = (kk[:, None] <= kk[None, :]).astype(np.float32)
    cst[:, 1, :] = 1.0
    cst[64, 2, :] = 1.0
    cst[:, 3, :] = np.eye(128, dtype=np.float32)
    in2 = []
    for ci, (b, r) in enumerate(cores):
        pa, pb = 2 * b, 2 * b + 1
        in2.append(dict(x1=res1[ci]["x1"], cvec=fm(c[b]), flags=flag_list[ci], flagt=flagt_list[ci], vecs=v1, consts=cst,
                        KTo=res1[ci]["KT"], Vo=res1[ci]["V"],
                        KTg=np.stack([res1[pa]["KT"], res1[pb]["KT"]]),
                        Vg=np.stack([res1[pa]["V"], res1[pb]["V"]]),
                        LFg=np.stack([res1[pa]["LF"], res1[pb]["LF"]]),
                        mix_ada_w=f32(mix_ada_w[1]), ffn_ada_w=f32(ffn_ada_w[1]),
                        attn_w_q=f32(attn_w_q[0]), attn_w_o=f32(attn_w_o[0]),
                        ffn_w_in=f32(ffn_w_in[1]), ffn_w_out=f32(ffn_w_out[1])))
    if "l1" not in _CACHE:
        _CACHE["l1"] = build_l1()
    res2 = run_bass_kernel_spmd(_CACHE["l1"], in2, core_ids=list(range(8))).results
    outp = np.zeros((B, S, D), np.float32)
    for ci, (b, r) in enumerate(cores):
        o = res2[ci]["out"]
        for t in range(NT):
            st = CH[r][t // 2] * 1024 + (t % 2) * TT
            outp[b, st:st + TT, :] = o[t].reshape(D, TT).T
    return outp
```
